# Optimizing a Trainium2 kernel written in Bass

```python
import jax, jax.numpy as jnp
from jax import lax
import numpy as np

D_MODEL = 1024
BATCH = 16
SEQ = 2048
DEPTH = 4

N_EVEN = (DEPTH + 1) // 2
N_ODD = DEPTH // 2

HEAD_DIM = 64
D_CONV_A = D_MODEL // 2
N_FOX_HEADS = (D_MODEL // 2) // HEAD_DIM
D_FOX = N_FOX_HEADS * HEAD_DIM
SHORT_CONV_WIDTH = 3
CONFORMER_CONV_WIDTH = 31
BLOCK_Q = 128
_FF_RAW = -(-8 * D_MODEL // 3)
D_FF = -(-_FF_RAW // 256) * 256
RMS_EPS = 1e-6
LN_EPS = 1e-5

OFF_GB = 0
OFF_GC = OFF_GB + D_CONV_A
OFF_XA = OFF_GC + D_CONV_A
OFF_Q = OFF_XA + D_CONV_A
OFF_K = OFF_Q + D_FOX
OFF_V = OFF_K + D_FOX
OFF_F = OFF_V + D_FOX
D_IN_EVEN = OFF_F + N_FOX_HEADS

kernel_name = "hybrid_shortconv_fox_conformer_swiglu"


def rms_norm(x, g):
    xf = x.astype(jnp.float32)
    y = xf * lax.rsqrt(jnp.mean(xf * xf, axis=-1, keepdims=True) + RMS_EPS)
    return (y * g.astype(jnp.float32)).astype(x.dtype)


def layer_norm(x, g, b):
    xf = x.astype(jnp.float32)
    mu = jnp.mean(xf, axis=-1, keepdims=True)
    xc = xf - mu
    var = jnp.mean(xc * xc, axis=-1, keepdims=True)
    y = xc * lax.rsqrt(var + LN_EPS)
    return (y * g.astype(jnp.float32) + b.astype(jnp.float32)).astype(x.dtype)


def causal_depthwise_conv(x, w):
    k_width, ch = w.shape
    return lax.conv_general_dilated(
        x, w[:, None, :].astype(x.dtype), window_strides=(1,),
        padding=[(k_width - 1, 0)], dimension_numbers=("NWC", "WIO", "NWC"),
        feature_group_count=ch)


def forgetting_attention(q, k, v, log_f):
    seq = q.shape[1]
    c = jnp.cumsum(log_f, axis=1).transpose(0, 2, 1)
    scale = HEAD_DIM ** -0.5
    outs = []
    for start in range(0, seq, BLOCK_Q):
        end = start + BLOCK_Q
        s = jnp.einsum("bqhd,bkhd->bhqk", q[:, start:end], k[:, :end],
                       preferred_element_type=jnp.float32) * scale
        bias = c[:, :, start:end, None] - c[:, :, None, :end]
        mask = (start + jnp.arange(BLOCK_Q))[:, None] >= jnp.arange(end)[None, :]
        s = jnp.where(mask, s + bias, -jnp.inf)
        p = jax.nn.softmax(s, axis=-1)
        outs.append(jnp.einsum("bhqk,bkhd->bqhd", p.astype(v.dtype), v[:, :end]))
    return jnp.concatenate(outs, axis=1)


def shortconv_fox_mixer(h, w_in, b_f, conv_w, w_out):
    bsz, seq, _ = h.shape
    proj = h @ w_in
    gate_b = proj[..., OFF_GB:OFF_GC]
    gate_c = proj[..., OFF_GC:OFF_XA]
    xa = proj[..., OFF_XA:OFF_Q]
    y_a = gate_b * causal_depthwise_conv(gate_c * xa, conv_w)
    q = proj[..., OFF_Q:OFF_K].reshape(bsz, seq, N_FOX_HEADS, HEAD_DIM)
    k = proj[..., OFF_K:OFF_V].reshape(bsz, seq, N_FOX_HEADS, HEAD_DIM)
    v = proj[..., OFF_V:OFF_F].reshape(bsz, seq, N_FOX_HEADS, HEAD_DIM)
    log_f = jax.nn.log_sigmoid((proj[..., OFF_F:] + b_f).astype(jnp.float32))
    y_b = forgetting_attention(q, k, v, log_f).reshape(bsz, seq, D_FOX)
    return jnp.concatenate([y_a, y_b.astype(h.dtype)], axis=-1) @ w_out


def conformer_conv_module(h, w_pw1, b_pw1, w_dw, b_dw, ln_g, ln_b, w_pw2, b_pw2):
    a, g = jnp.split(h @ w_pw1 + b_pw1, 2, axis=-1)
    u = a * jax.nn.sigmoid(g)
    u = causal_depthwise_conv(u, w_dw) + b_dw
    u = jax.nn.silu(layer_norm(u, ln_g, ln_b))
    return u @ w_pw2 + b_pw2


def swiglu_ffn(h, w_ffn_in, w_ffn_out):
    g, u = jnp.split(h @ w_ffn_in, 2, axis=-1)
    return (jax.nn.silu(g) * u) @ w_ffn_out


def setup_inputs(seed: int = 0) -> dict:
    key = jax.random.key(seed)
    ks = jax.random.split(key, 20)
    f32 = jnp.float32
    nrm = lambda k, shape, s: (jax.random.normal(k, shape, f32) * s)
    gain = lambda k, shape: 1.0 + 0.05 * jax.random.normal(k, shape, f32)
    return {
        "x": jax.random.normal(ks[0], (BATCH, SEQ, D_MODEL), f32),
        "norm_mix_pre": gain(ks[1], (DEPTH, D_MODEL)),
        "norm_mix_post": gain(ks[2], (DEPTH, D_MODEL)),
        "norm_ffn_pre": gain(ks[3], (DEPTH, D_MODEL)),
        "norm_ffn_post": gain(ks[4], (DEPTH, D_MODEL)),
        "w_in": nrm(ks[5], (N_EVEN, D_MODEL, D_IN_EVEN), D_MODEL ** -0.5),
        "b_forget": 1.0 + 3.0 * jax.random.uniform(ks[6], (N_EVEN, N_FOX_HEADS), f32),
        "w_short_conv": nrm(ks[7], (N_EVEN, SHORT_CONV_WIDTH, D_CONV_A), SHORT_CONV_WIDTH ** -0.5),
        "w_out": nrm(ks[8], (N_EVEN, D_CONV_A + D_FOX, D_MODEL), (D_CONV_A + D_FOX) ** -0.5),
        "w_pw1": nrm(ks[9], (N_ODD, D_MODEL, 2 * D_MODEL), D_MODEL ** -0.5),
        "b_pw1": nrm(ks[10], (N_ODD, 2 * D_MODEL), 0.02),
        "w_dw": nrm(ks[11], (N_ODD, CONFORMER_CONV_WIDTH, D_MODEL), CONFORMER_CONV_WIDTH ** -0.5),
        "b_dw": nrm(ks[12], (N_ODD, D_MODEL), 0.02),
        "ln_g": gain(ks[13], (N_ODD, D_MODEL)),
        "ln_b": nrm(ks[14], (N_ODD, D_MODEL), 0.02),
        "w_pw2": nrm(ks[15], (N_ODD, D_MODEL, D_MODEL), D_MODEL ** -0.5),
        "b_pw2": nrm(ks[16], (N_ODD, D_MODEL), 0.02),
        "w_ffn_in": nrm(ks[17], (DEPTH, D_MODEL, 2 * D_FF), D_MODEL ** -0.5),
        "w_ffn_out": nrm(ks[18], (DEPTH, D_FF, D_MODEL), D_FF ** -0.5),
    }


def reference(x, norm_mix_pre, norm_mix_post, norm_ffn_pre, norm_ffn_post,
              w_in, b_forget, w_short_conv, w_out,
              w_pw1, b_pw1, w_dw, b_dw, ln_g, ln_b, w_pw2, b_pw2,
              w_ffn_in, w_ffn_out):
    h = x
    for layer in range(DEPTH):
        j = layer // 2
        u = rms_norm(h, norm_mix_pre[layer])
        if layer % 2 == 0:
            m = shortconv_fox_mixer(u, w_in[j], b_forget[j], w_short_conv[j], w_out[j])
        else:
            m = conformer_conv_module(u, w_pw1[j], b_pw1[j], w_dw[j], b_dw[j],
                                      ln_g[j], ln_b[j], w_pw2[j], b_pw2[j])
        h = h + rms_norm(m, norm_mix_post[layer])
        f = swiglu_ffn(rms_norm(h, norm_ffn_pre[layer]), w_ffn_in[layer], w_ffn_out[layer])
        h = h + rms_norm(f, norm_ffn_post[layer])
    return h
```

```python
import numpy as np
from contextlib import ExitStack
import concourse.bass as bass
import concourse.mybir as mybir
from concourse.bass_utils import run_bass_kernel_spmd

F32 = mybir.dt.float32
BF16 = mybir.dt.bfloat16
AF = mybir.ActivationFunctionType
ALU = mybir.AluOpType
CELL = 256
DT_SIZE = {F32: 4, BF16: 2}
COMPUTE = ("pe", "act", "dve", "pool")
ALLENG = COMPUTE + ("sp",)

D = 1024
T = 2048
NB = 8
DEPTH = 4
DFF = 2816
NFB = 22
N_CORES = 8
SEQ_PER_CORE = 2
OFF_Q, OFF_K, OFF_V, OFF_F = 1536, 2048, 2560, 3072
RMS_EPS = 1e-6
LN_EPS = 1e-5
KW = 31
TH = 1024

G_MIXPRE, G_MIXPOST, G_FFNPRE, G_FFNPOST = 0, 32, 64, 96
P_WSC = 128
P_BP1A = 152
P_BP1G = 168
P_WDW = 184
P_BDW = 680
P_LNG = 696
P_LNB = 712
P_BP2 = 728
P_BF = 744
NPRM = 752


class View:
    __slots__ = ("ap", "keys", "region", "boff", "nbytes", "dtype", "p0", "p1")

    def __init__(self, ap, region, boff, nbytes, dtype, p0=0, p1=128):
        self.ap = ap
        self.region = region
        self.boff = boff
        self.nbytes = nbytes
        self.dtype = dtype
        self.p0 = p0
        self.p1 = p1
        c0 = boff // CELL
        c1 = (boff + nbytes - 1) // CELL
        gs = [g for g in (0, 1) if p0 < (g + 1) * 64 and p1 > g * 64]
        self.keys = tuple((region.name, c, g) for c in range(c0, c1 + 1) for g in gs)

    def cols(self, lo, hi):
        es = DT_SIZE[self.dtype]
        return View(self.ap[:, lo:hi], self.region, self.boff + lo * es, (hi - lo) * es, self.dtype, self.p0, self.p1)

    def parts(self, p0, p1):
        return View(self.ap[p0:p1], self.region, self.boff, self.nbytes, self.dtype, self.p0 + p0, self.p0 + p1)


class Region:
    def __init__(self, name, handle, nbytes):
        self.name = name
        self.handle = handle
        self.nbytes = nbytes

    def view(self, boff, n, dtype, p0=0, p1=128):
        es = DT_SIZE[dtype]
        assert boff % 4 == 0 and (n * es) % 4 == 0 and boff + n * es <= self.nbytes, (self.name, boff, n)
        if dtype == F32:
            ap = self.handle[p0:p1, boff // 4: boff // 4 + n]
        else:
            ap = self.handle[p0:p1, boff // 4: boff // 4 + (n * es) // 4].bitcast(dtype)
        return View(ap, self, boff, n * es, dtype, p0, p1)


class FakeView:
    def __init__(self, name):
        self.keys = ((name, 0, 0),)
        self.ap = None


class Op:
    __slots__ = ("eng", "fn", "idx", "waits", "signal", "sigval", "is_dma", "dsem", "dval", "epoch")


class Prog:
    def __init__(self, nc, n_dma_sems=8):
        self.nc = nc
        self.ops = {e: [] for e in ALLENG}
        self.last_w = {}
        self.readers = {}
        self.epoch = 0
        self.n_epochs = 1
        self.n_dma_sems = n_dma_sems
        self.dma_rr = {"sp": 0, "pool": 0, "act": 0}
        self.dma_count = {}
        self.dma_last = {}
        self.known = {e: {} for e in ALLENG}

    def new_epoch(self):
        self.epoch += 1
        self.n_epochs = self.epoch + 1

    def _deps(self, reads, writes):
        deps = set()
        lw = self.last_w
        for v in reads:
            for k in v.keys:
                w = lw.get(k)
                if w is not None:
                    deps.add(w)
        for v in writes:
            for k in v.keys:
                w = lw.get(k)
                if w is not None:
                    deps.add(w)
                r = self.readers.get(k)
                if r:
                    deps.update(r)
        return deps

    def _commit(self, op, reads, writes):
        for v in writes:
            for k in v.keys:
                self.last_w[k] = op
                self.readers[k] = []
        for v in reads:
            for k in v.keys:
                self.readers.setdefault(k, []).append(op)

    def _mk_waits(self, eng, deps, op):
        kn = self.known[eng]
        best = {}
        out = []
        for d in deps:
            if d is op:
                continue
            if d.is_dma:
                kk = ("dk", d.eng, d.dsem)
                if kn.get(kk, -1) >= d.dval:
                    continue
                kn[kk] = d.dval
                out.append(d)
            else:
                if d.eng == eng and eng == "pe":
                    continue
                b = best.get(d.eng)
                if b is None or d.idx > b.idx:
                    best[d.eng] = d
        for pe_, d in best.items():
            if kn.get(pe_, -1) >= d.idx:
                continue
            kn[pe_] = d.idx
            d.signal = True
            out.append(d)
        return out

    def op(self, eng, fn, reads=(), writes=()):
        o = Op()
        o.eng = eng
        o.fn = fn
        o.is_dma = False
        o.signal = False
        o.sigval = None
        o.epoch = self.epoch
        o.idx = len(self.ops[eng])
        o.waits = self._mk_waits(eng, self._deps(reads, writes), o)
        self.ops[eng].append(o)
        self._commit(o, reads, writes)
        return o

    def dma(self, queue, fn, reads=(), writes=()):
        o = Op()
        o.eng = queue
        o.fn = fn
        o.is_dma = True
        o.signal = False
        o.epoch = self.epoch
        o.idx = len(self.ops[queue])
        k = self.dma_rr[queue] % self.n_dma_sems
        self.dma_rr[queue] += 1
        o.dsem = k
        cnt = self.dma_count.get((queue, k), 0) + 1
        self.dma_count[(queue, k)] = cnt
        o.dval = 16 * cnt
        deps = self._deps(reads, writes)
        prev = self.dma_last.get((queue, k))
        if prev is not None:
            deps.add(prev)
        self.dma_last[(queue, k)] = o
        o.waits = self._mk_waits(queue, deps, o)
        self.ops[queue].append(o)
        self._commit(o, reads, writes)
        return o

    def emit(self):
        nc = self.nc
        with ExitStack() as st:
            esem = {}
            for e in ALLENG:
                used = set(o.epoch for o in self.ops[e] if (not o.is_dma) and o.signal)
                for ep in used:
                    esem[(e, ep)] = st.enter_context(nc.semaphore(f"s_{e}_{ep}"))
            dsem = {}
            for q in ("sp", "pool", "act"):
                for k in range(self.n_dma_sems):
                    if self.dma_count.get((q, k), 0) > 0:
                        dsem[(q, k)] = st.enter_context(nc.semaphore(f"d_{q}_{k}"))
            for e in ALLENG:
                cnt = {}
                for o in self.ops[e]:
                    if o.is_dma or not o.signal:
                        continue
                    c = cnt.get(o.epoch, 0) + 1
                    cnt[o.epoch] = c
                    o.sigval = c
            block = st.enter_context(nc.Block())

            def run(engname, eng):
                for o in self.ops[engname]:
                    for w in o.waits:
                        if w.is_dma:
                            eng.wait_ge(dsem[(w.eng, w.dsem)], w.dval)
                        else:
                            eng.wait_ge(esem[(w.eng, w.epoch)], w.sigval)
                    ins = o.fn(eng)
                    if o.is_dma:
                        ins.then_inc(dsem[(o.eng, o.dsem)], 16)
                    elif o.signal:
                        ins.then_inc(esem[(o.eng, o.epoch)], 1)

            @block.tensor
            def _(eng):
                run("pe", eng)

            @block.scalar
            def _(eng):
                run("act", eng)

            @block.vector
            def _(eng):
                run("dve", eng)

            @block.gpsimd
            def _(eng):
                run("pool", eng)

            @block.sync
            def _(eng):
                run("sp", eng)


def build_program(layers=(0, 1, 2, 3), n_seq=SEQ_PER_CORE, debug_out=False):
    nc = bass.Bass("TRN2", target_bir_lowering=False)
    n_even = 2
    dr = {}
    dr["xT"] = nc.dram_tensor("xT", [n_seq, D, T], F32, kind="ExternalInput").ap()
    dr["prm"] = nc.dram_tensor("prm", [128, NPRM], F32, kind="ExternalInput").ap()
    dr["w_sc"] = nc.dram_tensor("w_sc", [n_even, 4, 128, 8 * 384], F32, kind="ExternalInput").ap()
    dr["w_at"] = nc.dram_tensor("w_at", [n_even, 4, 128, 8 * 384], F32, kind="ExternalInput").ap()
    dr["w_fg"] = nc.dram_tensor("w_fg", [n_even, 128, 64], F32, kind="ExternalInput").ap()
    dr["w_o"] = nc.dram_tensor("w_o", [n_even, 8, 128, 8 * 128], F32, kind="ExternalInput").ap()
    dr["w_p1"] = nc.dram_tensor("w_p1", [2, 8, 128, 8 * 256], F32, kind="ExternalInput").ap()
    dr["w_p2"] = nc.dram_tensor("w_p2", [2, 8, 128, 8 * 128], F32, kind="ExternalInput").ap()
    dr["w_f1"] = nc.dram_tensor("w_f1", [DEPTH, NFB, 128, 8 * 256], F32, kind="ExternalInput").ap()
    dr["w_f2"] = nc.dram_tensor("w_f2", [DEPTH, 8, 128, NFB * 128], F32, kind="ExternalInput").ap()
    outT = nc.dram_tensor("outT", [n_seq, D, T], F32, kind="ExternalOutput").ap()

    H_B, A_B, B_B, M_B = 65536, 32768, 45056, 22528
    WSLOT = 6144
    NW = 3
    S_B = 27648
    with ExitStack() as st:
        hH = st.enter_context(nc.sbuf_tensor("regH", [128, H_B // 4], F32))
        hA = st.enter_context(nc.sbuf_tensor("regA", [128, A_B // 4], F32))
        hB = st.enter_context(nc.sbuf_tensor("regB", [128, B_B // 4], F32))
        hM = st.enter_context(nc.sbuf_tensor("regM", [128, M_B // 4], F32))
        hW = st.enter_context(nc.sbuf_tensor("regW", [128, NW * WSLOT // 4], F32))
        hS = st.enter_context(nc.sbuf_tensor("regS", [128, S_B // 4], F32))
        hP = st.enter_context(nc.psum_tensor("regP", [128, 8 * 512], F32))
        RH, RA, RB, RM = Region("H", hH, H_B), Region("A", hA, A_B), Region("B", hB, B_B), Region("M", hM, M_B)
        RW, RS, RP = Region("W", hW, NW * WSLOT), Region("S", hS, S_B), Region("P", hP, 16384)
        pr = Prog(nc)

        hT = [RH.view(j * 8192, T, F32) for j in range(NB)]
        uT = [RA.view(j * 4096, T, BF16) for j in range(NB)]
        uTh = [RA.view(j * 2048, TH, BF16) for j in range(NB)]
        mA = [RA.view(j * 4096, TH, F32) for j in range(NB)]
        mB = [RB.view(j * 4096, TH, F32) for j in range(NB)]
        actT = [RB.view(fb * 2048, TH, BF16) for fb in range(NFB)]
        mAf = lambda cb, ti: mA[cb].cols(ti * 512, (ti + 1) * 512)
        mMix = [RM.view(j * 2048, 512, F32) for j in range(NB)]
        mCf = lambda cb, ti: mA[cb].cols(0, 512) if ti == 0 else mMix[cb]
        yT = [RB.view(j * 4096, T, BF16) for j in range(NB)]
        vT = yT
        XG_STRIDE = 4352
        xglu = [RB.view(j * XG_STRIDE, T + 30, BF16) for j in range(NB)]
        g_hi = RB.view(32768, T, BF16, 0, 8)
        g_lo = RB.view(32768 + 4096, T, BF16, 0, 8)
        Qa = [RM.view(0, T, BF16), RM.view(4096, T, BF16)]
        Ka = [RM.view(8192, T, BF16), RM.view(12288, T, BF16)]
        Vaug = RM.view(16384, 16 * 192, BF16)
        Dj = [RM.view(0, KW * 128, BF16), RM.view(7936, KW * 128, BF16)]
        wslot = [RW.view(i * WSLOT, WSLOT // 2, BF16) for i in range(NW)]
        so = [0]

        def salloc(n, dtype, p0=0, p1=128, align=32):
            so[0] = ((so[0] + align - 1) // align) * align
            v = RS.view(so[0], n, dtype, p0, p1)
            so[0] += n * DT_SIZE[dtype]
            return v
        prm = salloc(NPRM, F32)
        hb = salloc(32, F32)
        nbf = salloc(2, F32)
        bgt = salloc(8, F32)
        ones_bf = salloc(128, BF16)
        ident_bf = salloc(128, BF16)
        maskb = salloc(128, BF16)
        one_col = salloc(8, F32)
        wfg = salloc(64, BF16)
        sq = [salloc(512, BF16, align=CELL) for _ in range(3)]
        rstd = [salloc(512, F32, align=CELL) for _ in range(2)]
        sA = [salloc(512, F32, align=CELL) for _ in range(4)]
        PT = [salloc(512, BF16, align=CELL) for _ in range(3)]
        zb = [salloc(520, F32, align=CELL) for _ in range(2)]
        zoff = [zb[0].boff, zb[1].boff]
        nmr = [RS.view(zoff[i], 512, F32) for i in range(2)]
        gsc = [RS.view(zoff[i], 512, F32, 0, 8) for i in range(2)]
        so[0] = ((so[0] + CELL - 1) // CELL) * CELL
        assert so[0] <= S_B, so[0]
        bank = [RP.view(b * 2048, 512, F32) for b in range(8)]
        PT6 = PT + [RB.view(40960 + i * 1024, 512, BF16) for i in range(3)]

        cnt = {"sq": 0, "sA": 0, "PT": 0, "w": 0, "rot": 0}

        def nxt(name, lst):
            v = lst[cnt[name] % len(lst)]
            cnt[name] += 1
            return v

        dma_tok = FakeView("dma_token")

        def load_w(src_ap, ncols):
            s = nxt("w", wslot)
            v = s.cols(0, ncols)
            pr.dma("pool", lambda e: e.dma_start(out=v.ap, in_=src_ap), reads=[dma_tok], writes=[v])
            return v

        def pcol(c):
            return prm.cols(c, c + 1)

        pr.dma("sp", lambda e: e.dma_start(out=prm.ap, in_=dr["prm"]), writes=[prm])
        pr.op("dve", lambda e: e.tensor_scalar(out=hb.ap, in0=prm.ap[:, P_BP1A:P_BP1A + 32], scalar1=0.5, scalar2=None,
                                               op0=ALU.mult), reads=[prm], writes=[hb])
        pr.op("dve", lambda e: e.tensor_scalar(out=nbf.ap, in0=prm.ap[:, P_BF:P_BF + 2], scalar1=-1.0, scalar2=None,
                                               op0=ALU.mult), reads=[prm], writes=[nbf])
        pr.op("dve", lambda e: e.memset(ones_bf.ap, 1.0), writes=[ones_bf])
        pr.op("dve", lambda e: e.memset(one_col.ap, 1.0), writes=[one_col])
        pr.op("dve", lambda e: e.memset(maskb.ap, 0.0), writes=[maskb])
        pr.op("pool", lambda e: e.affine_select(out=ident_bf.ap, in_=ones_bf.ap, pattern=[[1, 128]],
                                                compare_op=ALU.is_equal, fill=0.0, base=0, channel_multiplier=-1),
              reads=[ones_bf], writes=[ident_bf])
        pr.op("pool", lambda e: e.affine_select(out=maskb.ap, in_=maskb.ap, pattern=[[1, 128]],
                                                compare_op=ALU.is_ge, fill=-30000.0, base=0, channel_multiplier=-1),
              reads=[maskb], writes=[maskb])

        fillq = []

        def fill(n=1):
            while n > 0 and fillq:
                try:
                    next(fillq[0])
                    n -= 1
                except StopIteration:
                    fillq.pop(0)

        def drain():
            while fillq:
                fill(1000)

        def stats_rstd(sbank, out_rstd, eps, also_mean_bank=None, out_nmr=None):
            if also_mean_bank is None:
                t = nxt("sA", sA)
                pr.op("act", lambda e: e.activation(out=t.ap, in_=sbank.ap, func=AF.Ln, scale=1.0 / D, bias=eps),
                      reads=[sbank], writes=[t])
                pr.op("act", lambda e: e.activation(out=out_rstd.ap, in_=t.ap, func=AF.Exp, scale=-0.5),
                      reads=[t], writes=[out_rstd])
            else:
                mean = nxt("sA", sA)
                var = nxt("sA", sA)
                pr.op("act", lambda e: e.activation(out=mean.ap, in_=also_mean_bank.ap, func=AF.Copy, scale=1.0 / D),
                      reads=[also_mean_bank], writes=[mean])
                pr.op("dve", lambda e: e.tensor_tensor(out=var.ap, in0=mean.ap, in1=mean.ap, op=ALU.mult),
                      reads=[mean], writes=[var])
                pr.op("dve", lambda e: e.scalar_tensor_tensor(out=var.ap, in0=sbank.ap, scalar=1.0 / D, in1=var.ap,
                                                              op0=ALU.mult, op1=ALU.subtract),
                      reads=[sbank, var], writes=[var])
                pr.op("act", lambda e: e.activation(out=var.ap, in_=var.ap, func=AF.Ln, bias=eps),
                      reads=[var], writes=[var])
                pr.op("act", lambda e: e.activation(out=out_rstd.ap, in_=var.ap, func=AF.Exp, scale=-0.5),
                      reads=[var], writes=[out_rstd])
                pr.op("dve", lambda e: e.scalar_tensor_tensor(out=out_nmr.ap, in0=mean.ap, scalar=-1.0, in1=out_rstd.ap,
                                                              op0=ALU.mult, op1=ALU.mult),
                      reads=[mean, out_rstd], writes=[out_nmr])

        def prenorm_gen(gcol, t0, ntok, dst, sbanks=(6, 7), ring=None):
            for ti in range(ntok // 512):
                a, b = t0 + ti * 512, t0 + (ti + 1) * 512
                sb_ = bank[sbanks[ti % 2]]
                for j in range(NB):
                    s = nxt("sq", sq) if ring is None else nxt("PT", PT)
                    hv = hT[j].cols(a, b)
                    pr.op("act", lambda e, s=s, hv=hv: e.activation(out=s.ap, in_=hv.ap, func=AF.Square),
                          reads=[hv], writes=[s])
                    pr.op("pe", lambda e, s=s, j=j, sb_=sb_: e.matmul(sb_.ap, lhsT=ones_bf.ap, rhs=s.ap,
                                                                    start=(j == 0), stop=(j == NB - 1)),
                          reads=[ones_bf, s], writes=[sb_])
                    if j % 2 == 1:
                        yield
                r = rstd[ti % 2]
                stats_rstd(sb_, r, RMS_EPS)
                yield
                for j in range(NB):
                    hv = hT[j].cols(a, b)
                    dv = dst[j].cols(ti * 512, (ti + 1) * 512)
                    g = pcol(gcol + j)
                    pr.op("dve", lambda e, hv=hv, dv=dv, g=g, r=r: e.scalar_tensor_tensor(
                        out=dv.ap, in0=hv.ap, scalar=g.ap, in1=r.ap, op0=ALU.mult, op1=ALU.mult),
                        reads=[hv, g, r], writes=[dv])
                    if j % 2 == 1:
                        yield

        def prenorm(gcol, t0, ntok, dst):
            for _ in prenorm_gen(gcol, t0, ntok, dst):
                pass

        def outproj(src, nkb, wsrc, msb, gcol, t0, bias_col=None, bg=None, nrot=6, fill_n=0):
            stb = [bank[6], bank[7]]
            pending = []

            def flush_pair():
                grp = [pending.pop(0) for _ in range(min(2, len(pending)))]
                for s_, ti_, cb_ in reversed(grp):
                    pr.op("pe", lambda e, s_=s_, ti_=ti_, cb_=cb_: e.matmul(stb[ti_].ap, lhsT=ones_bf.ap, rhs=s_.ap,
                                                                           start=(cb_ == 0), stop=(cb_ == NB - 1)),
                          reads=[ones_bf, s_], writes=[stb[ti_]])
            for cb in range(NB):
                w = load_w(wsrc(cb), nkb * 128)
                g = pcol(gcol + cb)
                pbs = [bank[(cnt["rot"] + i_) % nrot] for i_ in range(2)]
                cnt["rot"] += 2
                for ti in range(2):
                    pb = pbs[ti]
                    for kb in range(nkb):
                        rv = src[kb].cols(ti * 512, (ti + 1) * 512)
                        wr = pbs if (ti == 0 and kb == 0) else [pb]
                        pr.op("pe", lambda e, pb=pb, w=w, kb=kb, rv=rv: e.matmul(
                            pb.ap, lhsT=w.ap[:, kb * 128:(kb + 1) * 128], rhs=rv.ap,
                            start=(kb == 0), stop=(kb == nkb - 1)), reads=[w, rv], writes=wr)
                    if ti == 0 and len(pending) >= 2:
                        flush_pair()
                    if fill_n:
                        fill(fill_n)
                    mv = msb(cb, ti)
                    s_ = nxt("sq", sq)
                    if bias_col is None:
                        pr.op("act", lambda e, s_=s_, pb=pb: e.activation(out=s_.ap, in_=pb.ap, func=AF.Square),
                              reads=[pb], writes=[s_])
                        pr.op("act", lambda e, mv=mv, pb=pb, g=g: e.activation(out=mv.ap, in_=pb.ap, func=AF.Copy, scale=g.ap),
                              reads=[pb, g], writes=[mv])
                    else:
                        bc = pcol(bias_col + cb)
                        bgc = bg.cols(cb, cb + 1)
                        pr.op("act", lambda e, s_=s_, pb=pb, bc=bc: e.activation(out=s_.ap, in_=pb.ap, func=AF.Square,
                                                                             bias=bc.ap),
                              reads=[pb, bc], writes=[s_])
                        pr.op("act", lambda e, mv=mv, pb=pb, bgc=bgc, g=g: e.activation(out=mv.ap, in_=pb.ap, func=AF.Identity,
                                                                                   scale=g.ap, bias=bgc.ap),
                              reads=[pb, bgc, g], writes=[mv])
                    pending.append((s_, ti, cb))
            while pending:
                flush_pair()
            if fill_n:
                drain()
            for ti in range(2):
                r = rstd[ti]
                stats_rstd(stb[ti], r, RMS_EPS)
                for cb in range(NB):
                    mv = msb(cb, ti)
                    hv = hT[cb].cols(t0 + ti * 512, t0 + (ti + 1) * 512)
                    pr.op("dve", lambda e, mv=mv, r=r: e.tensor_tensor(out=mv.ap, in0=mv.ap, in1=r.ap, op=ALU.mult),
                          reads=[mv, r], writes=[mv])
                    pr.op("dve", lambda e, mv=mv, hv=hv: e.tensor_tensor(out=hv.ap, in0=hv.ap, in1=mv.ap, op=ALU.add),
                          reads=[hv, mv], writes=[hv])

        uThM = [RM.view(j * 2048, TH, BF16) for j in range(NB)]

        def ffn_phase_b(l, usrc, hook=None):
            for fb in range(NFB):
                w = load_w(dr["w_f1"][l, fb], 8 * 256)
                for ti in range(2):
                    gb = bank[cnt["rot"] % 8]
                    ub = bank[(cnt["rot"] + 1) % 8]
                    cnt["rot"] += 2
                    for half, pb in ((0, gb), (1, ub)):
                        for kb in range(NB):
                            rv = usrc[kb].cols(ti * 512, (ti + 1) * 512)
                            wr = [gb, ub] if (half == 0 and kb == 0) else [pb]
                            pr.op("pe", lambda e, pb=pb, w=w, kb=kb, rv=rv, half=half: e.matmul(
                                pb.ap, lhsT=w.ap[:, kb * 256 + half * 128: kb * 256 + half * 128 + 128], rhs=rv.ap,
                                start=(kb == 0), stop=(kb == NB - 1)), reads=[w, rv], writes=wr)
                    s_ = nxt("sA", sA)
                    pr.op("act", lambda e, s_=s_, gb=gb: e.activation(out=s_.ap, in_=gb.ap, func=AF.Silu),
                          reads=[gb], writes=[s_])
                    av = actT[fb].cols(ti * 512, (ti + 1) * 512)
                    pr.op("dve", lambda e, av=av, ub=ub, s_=s_: e.tensor_tensor(out=av.ap, in0=ub.ap, in1=s_.ap, op=ALU.mult),
                          reads=[ub, s_], writes=[av])
                if hook is not None and fb == 5:
                    hook()

        mFf = lambda cb, ti: mMix[cb] if ti == 0 else RA.view(16384 + cb * 2048, 512, F32)

        def ffn_prenorm_filler(l):
            return prenorm_gen(G_FFNPRE + l * 8, 0, TH, uThM, sbanks=(4, 5), ring="PT")

        def ffn_layer(l, after_h0=None):
            cnt["rot"] = 0
            ffn_phase_b(l, uThM, hook=lambda: prenorm(G_FFNPRE + l * 8, TH, TH, uTh))
            cnt["rot"] = 0
            outproj(actT, NFB, lambda cb: dr["w_f2"][l, cb], mFf, G_FFNPOST + l * 8, 0)
            if after_h0 is not None:
                after_h0()
            cnt["rot"] = 0
            ffn_phase_b(l, uTh)
            cnt["rot"] = 0
            outproj(actT, NFB, lambda cb: dr["w_f2"][l, cb], mAf, G_FFNPOST + l * 8, TH)

        def even_mixer(l):
            le = l // 2
            prenorm(G_MIXPRE + l * 8, 0, T, uT)
            for j in range(4):
                w = load_w(dr["w_sc"][le, j], 8 * 384)
                for ti in range(4):
                    bb, cbk, xb = bank[0 + 3 * (ti % 2)], bank[1 + 3 * (ti % 2)], bank[2 + 3 * (ti % 2)]
                    for part, pb in ((0, bb), (1, cbk), (2, xb)):
                        for kb in range(NB):
                            rv = uT[kb].cols(ti * 512, (ti + 1) * 512)
                            wr = [bb, cbk, xb] if (part == 0 and kb == 0) else [pb]
                            pr.op("pe", lambda e, pb=pb, w=w, kb=kb, rv=rv, part=part: e.matmul(
                                pb.ap, lhsT=w.ap[:, kb * 384 + part * 128: kb * 384 + part * 128 + 128], rhs=rv.ap,
                                start=(kb == 0), stop=(kb == NB - 1)), reads=[w, rv], writes=wr)
                    cs = nxt("sA", sA)
                    pr.op("act", lambda e, cs=cs, cbk=cbk: e.activation(out=cs.ap, in_=cbk.ap, func=AF.Copy),
                          reads=[cbk], writes=[cs])
                    z = zb[ti % 2]
                    zp = zb[(ti + 1) % 2]
                    if ti == 0:
                        zh = z.cols(0, 2)
                        pr.op("dve", lambda e, zh=zh: e.memset(zh.ap, 0.0), writes=[zh])
                    else:
                        zh = z.cols(0, 2)
                        zt = zp.cols(512, 514)
                        pr.op("dve", lambda e, zh=zh, zt=zt: e.tensor_copy(out=zh.ap, in_=zt.ap), reads=[zt], writes=[zh])
                    zm = z.cols(2, 514)
                    pr.op("dve", lambda e, zm=zm, xb=xb, cs=cs: e.tensor_tensor(out=zm.ap, in0=xb.ap, in1=cs.ap, op=ALU.mult),
                          reads=[xb, cs], writes=[zm])
                    wc = [pcol(P_WSC + (le * 4 + j) * 3 + k) for k in range(3)]
                    accv = nxt("sA", sA)
                    z2, z1, z0 = z.cols(2, 514), z.cols(1, 513), z.cols(0, 512)
                    pr.op("dve", lambda e, z2=z2, wc=wc, accv=accv: e.tensor_scalar(out=accv.ap, in0=z2.ap, scalar1=wc[2].ap, scalar2=None,
                                                                       op0=ALU.mult), reads=[z2, wc[2]], writes=[accv])
                    pr.op("dve", lambda e, z1=z1, wc=wc, accv=accv: e.scalar_tensor_tensor(out=accv.ap, in0=z1.ap, scalar=wc[1].ap, in1=accv.ap,
                                                                              op0=ALU.mult, op1=ALU.add),
                          reads=[z1, wc[1], accv], writes=[accv])
                    pr.op("dve", lambda e, z0=z0, wc=wc, accv=accv: e.scalar_tensor_tensor(out=accv.ap, in0=z0.ap, scalar=wc[0].ap, in1=accv.ap,
                                                                              op0=ALU.mult, op1=ALU.add),
                          reads=[z0, wc[0], accv], writes=[accv])
                    yv = yT[j].cols(ti * 512, (ti + 1) * 512)
                    pr.op("dve", lambda e, yv=yv, bb=bb, accv=accv: e.tensor_tensor(out=yv.ap, in0=bb.ap, in1=accv.ap, op=ALU.mult),
                          reads=[bb, accv], writes=[yv])
            wsrc = dr["w_fg"][le]
            pr.dma("pool", lambda e: e.dma_start(out=wfg.ap, in_=wsrc), reads=[dma_tok], writes=[wfg])
            prev = None
            for ti in range(4):
                fbk = bank[6 + (ti % 2)]
                fv = fbk.parts(0, 8)
                for kb in range(NB):
                    rv = uT[kb].cols(ti * 512, (ti + 1) * 512)
                    pr.op("pe", lambda e, fv=fv, kb=kb, rv=rv: e.matmul(fv.ap, lhsT=wfg.ap[:, kb * 8:(kb + 1) * 8], rhs=rv.ap,
                                                                      start=(kb == 0), stop=(kb == NB - 1)),
                          reads=[wfg, rv], writes=[fbk])
                lt = gsc[ti % 2]
                nb_ = nbf.parts(0, 8).cols(le, le + 1)
                pr.op("act", lambda e, lt=lt, fv=fv, nb_=nb_: e.activation(out=lt.ap, in_=fv.ap, func=AF.Exp, scale=-1.0, bias=nb_.ap),
                      reads=[fbk, nb_], writes=[lt])
                pr.op("act", lambda e, lt=lt: e.activation(out=lt.ap, in_=lt.ap, func=AF.Ln, bias=1.0), reads=[lt], writes=[lt])
                ones8 = one_col.parts(0, 8).cols(0, 1)
                if prev is None:
                    pr.op("dve", lambda e, lt=lt, ones8=ones8: e.tensor_tensor_scan(
                        out=lt.ap, data0=ones8.ap.to_broadcast([8, 512]), data1=lt.ap, initial=0.0,
                        op0=ALU.mult, op1=ALU.add), reads=[lt, ones8], writes=[lt])
                else:
                    pv = prev.cols(511, 512)
                    pr.op("dve", lambda e, lt=lt, ones8=ones8, pv=pv: e.tensor_tensor_scan(
                        out=lt.ap, data0=ones8.ap.to_broadcast([8, 512]), data1=lt.ap, initial=pv.ap,
                        op0=ALU.mult, op1=ALU.add), reads=[lt, ones8, pv], writes=[lt])
                hv = g_hi.cols(ti * 512, (ti + 1) * 512)
                lv = g_lo.cols(ti * 512, (ti + 1) * 512)
                pr.op("dve", lambda e, hv=hv, lt=lt: e.tensor_copy(out=hv.ap, in_=lt.ap), reads=[lt], writes=[hv])
                pr.op("dve", lambda e, lv=lv, hv=hv, lt=lt: e.tensor_tensor(out=lv.ap, in0=lt.ap, in1=hv.ap, op=ALU.subtract),
                      reads=[lt, hv], writes=[lv])
                prev = lt
            for j in range(4):
                w = load_w(dr["w_at"][le, j], 8 * 384)
                for tl, cval in ((Qa[1], 1.0), (Ka[1], -1.0)):
                    tz = tl.parts(0, 64)
                    pr.op("dve", lambda e, tz=tz: e.memset(tz.ap, 0.0), writes=[tz])
                    t4 = tl.parts(0, 4)
                    pr.op("dve", lambda e, t4=t4, cval=cval: e.memset(t4.ap, cval), writes=[t4])
                for tl, cval in ((Qa[0], 1.0), (Ka[0], -1.0)):
                    t4 = tl.parts(64, 68)
                    pr.op("dve", lambda e, t4=t4, cval=cval: e.memset(t4.ap, cval), writes=[t4])
                for hh in range(2):
                    hidx = 2 * j + hh
                    r0 = 64 if hh == 0 else 0
                    hrow = g_hi.parts(hidx, hidx + 1)
                    lrow = g_lo.parts(hidx, hidx + 1)
                    q0 = Qa[hh].parts(r0, r0 + 1)
                    q1 = Qa[hh].parts(r0 + 1, r0 + 2)
                    k2 = Ka[hh].parts(r0 + 2, r0 + 3)
                    k3 = Ka[hh].parts(r0 + 3, r0 + 4)
                    pr.dma("sp", lambda e, o=q0, i=hrow: e.dma_start(out=o.ap, in_=i.ap), reads=[hrow], writes=[q0, dma_tok])
                    pr.dma("sp", lambda e, o=q1, i=lrow: e.dma_start(out=o.ap, in_=i.ap), reads=[lrow], writes=[q1, dma_tok])
                    pr.dma("sp", lambda e, o=k2, i=hrow: e.dma_start(out=o.ap, in_=i.ap), reads=[hrow], writes=[k2, dma_tok])
                    pr.dma("sp", lambda e, o=k3, i=lrow: e.dma_start(out=o.ap, in_=i.ap), reads=[lrow], writes=[k3, dma_tok])
                for part, dst, scl in ((0, Qa, 0.125), (1, Ka, 1.0)):
                    for ti in range(4):
                        pb = bank[cnt["rot"] % 6]
                        cnt["rot"] += 1
                        for kb in range(NB):
                            rv = uT[kb].cols(ti * 512, (ti + 1) * 512)
                            pr.op("pe", lambda e, pb=pb, w=w, kb=kb, rv=rv, part=part: e.matmul(
                                pb.ap, lhsT=w.ap[:, kb * 384 + part * 128: kb * 384 + part * 128 + 128], rhs=rv.ap,
                                start=(kb == 0), stop=(kb == NB - 1)), reads=[w, rv], writes=[pb])
                        d0 = dst[0].cols(ti * 512, (ti + 1) * 512).parts(0, 64)
                        d1 = dst[1].cols(ti * 512, (ti + 1) * 512).parts(64, 128)
                        pr.op("dve", lambda e, d0=d0, pb=pb, scl=scl: e.tensor_scalar(out=d0.ap, in0=pb.ap[0:64], scalar1=scl,
                                                                                  scalar2=None, op0=ALU.mult),
                              reads=[pb], writes=[d0])
                        pr.op("dve", lambda e, d1=d1, pb=pb, scl=scl: e.tensor_scalar(out=d1.ap, in0=pb.ap[64:128], scalar1=scl,
                                                                                  scalar2=None, op0=ALU.mult),
                              reads=[pb], writes=[d1])
                vo = RM.view(16384, 16 * 192, BF16)
                v3 = vo.ap.rearrange("p (t c) -> p t c", c=192)
                pr.op("dve", lambda e, v3=v3: e.memset(v3[:, :, 64:128], 1.0), writes=[vo])
                for tg in range(4):
                    pb = bank[cnt["rot"] % 6]
                    cnt["rot"] += 1
                    for tb4 in range(4):
                        tb = tg * 4 + tb4
                        for kb in range(NB):
                            lv_ = uT[kb].cols(tb * 128, (tb + 1) * 128)
                            pr.op("pe", lambda e, pb=pb, w=w, kb=kb, lv_=lv_, tb4=tb4: e.matmul(
                                pb.ap[:, tb4 * 128:(tb4 + 1) * 128], lhsT=lv_.ap, rhs=w.ap[:, kb * 384 + 256: kb * 384 + 384],
                                start=(kb == 0), stop=(kb == NB - 1)), reads=[w, lv_], writes=[pb])
                    p3 = pb.ap.rearrange("p (t c) -> p t c", c=128)
                    pr.op("dve", lambda e, v3=v3, p3=p3, tg=tg: e.tensor_copy(out=v3[:, tg * 4:(tg + 1) * 4, 0:64], in_=p3[:, :, 0:64]),
                          reads=[pb], writes=[vo])
                    pr.op("dve", lambda e, v3=v3, p3=p3, tg=tg: e.tensor_copy(out=v3[:, tg * 4:(tg + 1) * 4, 128:192], in_=p3[:, :, 64:128]),
                          reads=[pb], writes=[vo])
                tasks = []
                for hh in range(2):
                    for qt in range(4):
                        nkb = 4 * qt + 4
                        grs = [list(range(a_, min(a_ + 3, nkb))) for a_ in range(0, nkb, 3)]
                        for gi, kbs in enumerate(grs):
                            tasks.append((hh, qt, nkb, kbs, gi == len(grs) - 1))
                tinfo = {}

                def s_task(ti_):
                    hh, qt, nkb, kbs, _ = tasks[ti_]
                    rows = (0, 68) if hh == 0 else (0, 128)
                    qa = Qa[hh].parts(*rows)
                    ka = Ka[hh].parts(*rows)
                    banks_ = [bank[(ti_ % 2) * 3 + ii] for ii in range(len(kbs))]
                    info = []
                    for ii, kb in enumerate(kbs):
                        i = kb - 4 * qt
                        c0 = max(0, i) * 128
                        sbk = banks_[ii]
                        sv = sbk.cols(c0, 512)
                        kv = ka.cols(kb * 128, (kb + 1) * 128)
                        qv = qa.cols(qt * 512 + c0, (qt + 1) * 512)
                        wr = banks_ if ii == 0 else [sbk]
                        pr.op("pe", lambda e, sv=sv, kv=kv, qv=qv, i=i: e.matmul(sv.ap, lhsT=kv.ap, rhs=qv.ap, start=True, stop=(i < 0)),
                              reads=[kv, qv], writes=wr)
                        if i >= 0:
                            dv = sbk.cols(c0, c0 + 128)
                            pr.op("pe", lambda e, dv=dv: e.matmul(dv.ap, lhsT=ident_bf.ap, rhs=maskb.ap, start=False, stop=True),
                                  reads=[ident_bf, maskb], writes=[sbk])
                        info.append((kb, c0, sbk))
                    tinfo[ti_] = info

                def pv_task(ti_):
                    hh, qt, nkb, kbs, last = tasks[ti_]
                    info = tinfo.pop(ti_)
                    vcol0 = 0 if hh == 0 else 64
                    ob = bank[6 + ((hh * 4 + qt) % 2)]
                    pts = []
                    for ii, (kb, c0, sbk) in enumerate(info):
                        pt = PT6[(ti_ % 2) * 3 + ii]
                        pv_ = pt.cols(c0, 512)
                        sv = sbk.cols(c0, 512)
                        pr.op("act", lambda e, pv_=pv_, sv=sv: e.activation(out=pv_.ap, in_=sv.ap, func=AF.Exp),
                              reads=[sbk], writes=[pv_])
                        pts.append(pv_)
                    for ii, (kb, c0, sbk) in enumerate(info):
                        ov = ob.cols(c0, 512)
                        vv = Vaug.cols(kb * 192 + vcol0, kb * 192 + vcol0 + 128)
                        rd = [vv] + (pts if ii == 0 else [pts[ii]])
                        pr.op("pe", lambda e, ov=ov, vv=vv, p_=pts[ii], kb=kb, nkb=nkb: e.matmul(
                            ov.ap, lhsT=vv.ap, rhs=p_.ap, start=(kb == 0), stop=(kb == nkb - 1)),
                            reads=rd, writes=[ob])
                    if last:
                        rc = nxt("sA", sA)
                        yb = yT[4 + j].cols(qt * 512, (qt + 1) * 512)
                        rs0, rs1 = (64, 128) if hh == 0 else (0, 64)
                        o0, o1 = (0, 64) if hh == 0 else (64, 128)
                        rcv = rc.parts(rs0, rs1)
                        pr.op("dve", lambda e, rcv=rcv, ob=ob, rs0=rs0, rs1=rs1: e.reciprocal(out=rcv.ap, in_=ob.ap[rs0:rs1]),
                              reads=[ob], writes=[rcv])
                        yv = yb.parts(o0, o1)
                        pr.op("dve", lambda e, yv=yv, ob=ob, rcv=rcv, o0=o0, o1=o1: e.tensor_tensor(out=yv.ap, in0=ob.ap[o0:o1], in1=rcv.ap, op=ALU.mult),
                              reads=[ob, rcv], writes=[yv])

                s_task(0)
                for ti_ in range(len(tasks)):
                    if ti_ + 1 < len(tasks):
                        s_task(ti_ + 1)
                    pv_task(ti_)
            for half in range(2):
                cnt["rot"] = 0
                src = [yT[kb].cols(half * TH, (half + 1) * TH) for kb in range(NB)]
                if half == 0:
                    outproj(src, NB, lambda cb: dr["w_o"][le, cb], mAf, G_MIXPOST + l * 8, 0)
                else:
                    fillq.append(ffn_prenorm_filler(l))
                    outproj(src, NB, lambda cb: dr["w_o"][le, cb], mAf, G_MIXPOST + l * 8, TH, nrot=4, fill_n=2)

        def odd_mixer(l):
            lo = l // 2
            prenorm(G_MIXPRE + l * 8, 0, T, uT)
            cnt["rot"] = 0
            for j in range(NB):
                w = load_w(dr["w_p1"][lo, j], 8 * 256)
                xz = xglu[j].cols(0, 30)
                pr.op("dve", lambda e, xz=xz: e.memset(xz.ap, 0.0), writes=[xz])
                for ti in range(4):
                    ab = bank[cnt["rot"] % 8]
                    gb = bank[(cnt["rot"] + 1) % 8]
                    cnt["rot"] += 2
                    for half, pb in ((0, ab), (1, gb)):
                        for kb in range(NB):
                            rv = uT[kb].cols(ti * 512, (ti + 1) * 512)
                            wr = [ab, gb] if (half == 0 and kb == 0) else [pb]
                            pr.op("pe", lambda e, pb=pb, w=w, kb=kb, rv=rv, half=half: e.matmul(
                                pb.ap, lhsT=w.ap[:, kb * 256 + half * 128: kb * 256 + half * 128 + 128], rhs=rv.ap,
                                start=(kb == 0), stop=(kb == NB - 1)), reads=[w, rv], writes=wr)
                    tt = nxt("sA", sA)
                    aa = nxt("sA", sA)
                    hg = hb.cols(16 + lo * 8 + j, 16 + lo * 8 + j + 1)
                    ha = hb.cols(lo * 8 + j, lo * 8 + j + 1)
                    pr.op("act", lambda e, tt=tt, gb=gb, hg=hg: e.activation(out=tt.ap, in_=gb.ap, func=AF.Tanh, scale=0.5, bias=hg.ap),
                          reads=[gb, hg], writes=[tt])
                    pr.op("act", lambda e, aa=aa, ab=ab, ha=ha: e.activation(out=aa.ap, in_=ab.ap, func=AF.Identity, scale=0.5, bias=ha.ap),
                          reads=[ab, ha], writes=[aa])
                    xv = xglu[j].cols(30 + ti * 512, 30 + (ti + 1) * 512)
                    pr.op("dve", lambda e, xv=xv, tt=tt, aa=aa: e.scalar_tensor_tensor(out=xv.ap, in0=tt.ap, scalar=1.0, in1=aa.ap,
                                                                                   op0=ALU.add, op1=ALU.mult),
                          reads=[tt, aa], writes=[xv])
            cnt["rot"] = 0

            def conv_unit(j, ti, dj, bd):
                pb = bank[cnt["rot"] % 4]
                cnt["rot"] += 1
                for k in range(KW):
                    dk = dj.cols(k * 128, (k + 1) * 128)
                    xv = xglu[j].cols(ti * 512 + k, ti * 512 + k + 512)
                    pr.op("pe", lambda e, pb=pb, dk=dk, xv=xv, k=k: e.matmul(pb.ap, lhsT=dk.ap, rhs=xv.ap,
                                                                         start=(k == 0), stop=(k == KW - 1)),
                          reads=[dk, xv], writes=[pb])
                yv = uT[j].cols(ti * 512, (ti + 1) * 512)
                pr.op("act", lambda e, yv=yv, pb=pb, bd=bd: e.activation(out=yv.ap, in_=pb.ap, func=AF.Identity, bias=bd.ap),
                      reads=[pb, bd], writes=[yv])

            def ln_gen(ti):
                s1 = bank[4]
                s2 = bank[5]
                for j in range(NB):
                    yv = uT[j].cols(ti * 512, (ti + 1) * 512)
                    s_ = nxt("PT", PT)
                    pr.op("act", lambda e, s_=s_, yv=yv: e.activation(out=s_.ap, in_=yv.ap, func=AF.Square), reads=[yv], writes=[s_])
                    pr.op("pe", lambda e, yv=yv, j=j: e.matmul(s1.ap, lhsT=ones_bf.ap, rhs=yv.ap, start=(j == 0), stop=(j == NB - 1)),
                          reads=[ones_bf, yv], writes=[s1])
                    pr.op("pe", lambda e, s_=s_, j=j: e.matmul(s2.ap, lhsT=ones_bf.ap, rhs=s_.ap, start=(j == 0), stop=(j == NB - 1)),
                          reads=[ones_bf, s_], writes=[s2])
                    if j % 2 == 1:
                        yield
                r = rstd[ti % 2]
                nm = nmr[ti % 2]
                stats_rstd(s2, r, LN_EPS, also_mean_bank=s1, out_nmr=nm)
                yield
                for j in range(NB):
                    yv = uT[j].cols(ti * 512, (ti + 1) * 512)
                    t1 = nxt("sA", sA)
                    pr.op("dve", lambda e, t1=t1, yv=yv, r=r: e.tensor_tensor(out=t1.ap, in0=yv.ap, in1=r.ap, op=ALU.mult),
                          reads=[yv, r], writes=[t1])
                    pr.op("dve", lambda e, t1=t1, nm=nm: e.tensor_tensor(out=t1.ap, in0=t1.ap, in1=nm.ap, op=ALU.add),
                          reads=[t1, nm], writes=[t1])
                    vv = vT[j].cols(ti * 512, (ti + 1) * 512)
                    lg = pcol(P_LNG + lo * 8 + j)
                    lb = pcol(P_LNB + lo * 8 + j)
                    pr.op("act", lambda e, vv=vv, t1=t1, lg=lg, lb=lb: e.activation(out=vv.ap, in_=t1.ap, func=AF.Silu, scale=lg.ap, bias=lb.ap),
                          reads=[t1, lg, lb], writes=[vv])
                    if j % 2 == 1:
                        yield

            for j in range(NB):
                dj = Dj[j % 2]
                for k in range(KW):
                    dk = dj.cols(k * 128, (k + 1) * 128)
                    wc = pcol(P_WDW + (lo * 8 + j) * KW + k)
                    pr.op("act", lambda e, dk=dk, wc=wc: e.activation(out=dk.ap, in_=ident_bf.ap, func=AF.Copy, scale=wc.ap),
                          reads=[ident_bf, wc], writes=[dk])
                bd = pcol(P_BDW + lo * 8 + j)
                for ti in range(4):
                    conv_unit(j, ti, dj, bd)
                    if j == NB - 1 and ti < 2:
                        for _ in ln_gen(ti):
                            pass
            pr.op("dve", lambda e: e.tensor_tensor(out=bgt.ap, in0=prm.ap[:, P_BP2 + lo * 8:P_BP2 + lo * 8 + 8],
                                                   in1=prm.ap[:, G_MIXPOST + l * 8:G_MIXPOST + l * 8 + 8], op=ALU.mult),
                  reads=[prm], writes=[bgt])
            fillq.append(ln_gen(2))
            fillq.append(ln_gen(3))
            for half in range(2):
                cnt["rot"] = 0
                src = [vT[kb].cols(half * TH, (half + 1) * TH) for kb in range(NB)]
                if half == 0:
                    outproj(src, NB, lambda cb: dr["w_p2"][lo, cb], mCf, G_MIXPOST + l * 8, 0, bias_col=P_BP2 + lo * 8, bg=bgt,
                            nrot=4, fill_n=2)
                else:
                    fillq.append(ffn_prenorm_filler(l))
                    outproj(src, NB, lambda cb: dr["w_p2"][lo, cb], mAf, G_MIXPOST + l * 8, TH, bias_col=P_BP2 + lo * 8, bg=bgt,
                            nrot=4, fill_n=2)

        outs = []

        def load_x(s_, half):
            for j in range(NB):
                hv = hT[j].cols(half * TH, (half + 1) * TH)
                src = dr["xT"][s_, j * 128:(j + 1) * 128, half * TH:(half + 1) * TH]
                pr.dma("sp", lambda e, hv=hv, src=src: e.dma_start(out=hv.ap, in_=src), writes=[hv])

        def store_out(s_, half):
            for j in range(NB):
                hv = hT[j].cols(half * TH, (half + 1) * TH)
                dst = outT[s_, j * 128:(j + 1) * 128, half * TH:(half + 1) * TH]
                fv = FakeView(f"out{s_}_{j}_{half}")
                outs.append(fv)
                pr.dma("sp", lambda e, hv=hv, dst=dst: e.dma_start(out=dst, in_=hv.ap), reads=[hv], writes=[fv])

        for t4 in range(4):
            for j in range(NB):
                hv = hT[j].cols(t4 * 512, (t4 + 1) * 512)
                src = dr["xT"][0, j * 128:(j + 1) * 128, t4 * 512:(t4 + 1) * 512]
                pr.dma("sp", lambda e, hv=hv, src=src: e.dma_start(out=hv.ap, in_=src), writes=[hv])
        for s in range(n_seq):
            for li, l in enumerate(layers):
                pr.new_epoch()
                cnt["rot"] = 0
                if l % 2 == 0:
                    even_mixer(l)
                else:
                    odd_mixer(l)
                if li == len(layers) - 1:
                    def after_h0(s=s):
                        store_out(s, 0)
                        if s + 1 < n_seq:
                            load_x(s + 1, 0)
                    ffn_layer(l, after_h0=after_h0)
                else:
                    ffn_layer(l)
            if s + 1 < n_seq:
                store_out(s, 1)
                load_x(s + 1, 1)
            else:
                for t4 in (2, 3):
                    for j in range(NB):
                        hv = hT[j].cols(t4 * 512, (t4 + 1) * 512)
                        dst = outT[s, j * 128:(j + 1) * 128, t4 * 512:(t4 + 1) * 512]
                        fv = FakeView(f"out{s}_{j}_t{t4}")
                        outs.append(fv)
                        pr.dma("sp", lambda e, hv=hv, dst=dst: e.dma_start(out=dst, in_=hv.ap), reads=[hv], writes=[fv])
        pr.op("sp", lambda e: e.nop(), reads=outs)
        pr.emit()
    return nc


def _blk(v):
    return np.ascontiguousarray(np.moveaxis(v.reshape(v.shape[:-1] + (8, 128)), -1, 0))


def prep_shared(inp):
    f = np.float32
    prm = np.zeros((128, NPRM), f)
    for off, key in ((G_MIXPRE, "norm_mix_pre"), (G_MIXPOST, "norm_mix_post"), (G_FFNPRE, "norm_ffn_pre"),
                     (G_FFNPOST, "norm_ffn_post")):
        prm[:, off:off + 32] = _blk(np.asarray(inp[key], f)).reshape(128, 32)
    wsc = np.asarray(inp["w_short_conv"], f)
    t = wsc.reshape(2, 3, 4, 128).transpose(3, 0, 2, 1)
    prm[:, P_WSC:P_WSC + 24] = t.reshape(128, 24)
    bp1 = np.asarray(inp["b_pw1"], f)
    prm[:, P_BP1A:P_BP1A + 16] = _blk(bp1[:, :1024]).reshape(128, 16)
    prm[:, P_BP1G:P_BP1G + 16] = _blk(bp1[:, 1024:]).reshape(128, 16)
    wdw = np.asarray(inp["w_dw"], f)
    t = wdw.reshape(2, KW, 8, 128).transpose(3, 0, 2, 1)
    prm[:, P_WDW:P_WDW + 2 * 8 * KW] = t.reshape(128, 2 * 8 * KW)
    for off, key in ((P_BDW, "b_dw"), (P_LNG, "ln_g"), (P_LNB, "ln_b"), (P_BP2, "b_pw2")):
        prm[:, off:off + 16] = _blk(np.asarray(inp[key], f)).reshape(128, 16)
    bf = np.asarray(inp["b_forget"], f)
    prm[0:8, P_BF:P_BF + 2] = bf.T

    def chunk_rows(w):
        K, n = w.shape
        return np.ascontiguousarray(w.reshape(K // 128, 128, n).transpose(1, 0, 2)).reshape(128, (K // 128) * n)

    w_in = np.asarray(inp["w_in"], f)
    w_sc = np.empty((2, 4, 128, 8 * 384), f)
    w_at = np.empty((2, 4, 128, 8 * 384), f)
    w_fg = np.empty((2, 128, 64), f)
    for le in range(2):
        for j in range(4):
            cols = np.concatenate([np.arange(j * 128, (j + 1) * 128), np.arange(512 + j * 128, 512 + (j + 1) * 128),
                                   np.arange(1024 + j * 128, 1024 + (j + 1) * 128)])
            w_sc[le, j] = chunk_rows(w_in[le][:, cols])
            cols = np.concatenate([np.arange(OFF_Q + j * 128, OFF_Q + (j + 1) * 128), np.arange(OFF_K + j * 128, OFF_K + (j + 1) * 128),
                                   np.arange(OFF_V + j * 128, OFF_V + (j + 1) * 128)])
            w_at[le, j] = chunk_rows(w_in[le][:, cols])
        w_fg[le] = chunk_rows(w_in[le][:, OFF_F:OFF_F + 8])
    w_out = np.asarray(inp["w_out"], f)
    w_o = np.stack([np.stack([chunk_rows(w_out[le][:, cb * 128:(cb + 1) * 128]) for cb in range(8)]) for le in range(2)])
    w_pw1 = np.asarray(inp["w_pw1"], f)
    w_p1 = np.stack([np.stack([chunk_rows(np.concatenate([w_pw1[lo][:, j * 128:(j + 1) * 128],
                                                          w_pw1[lo][:, 1024 + j * 128:1024 + (j + 1) * 128]], 1))
                               for j in range(8)]) for lo in range(2)])
    w_pw2 = np.asarray(inp["w_pw2"], f)
    w_p2 = np.stack([np.stack([chunk_rows(w_pw2[lo][:, cb * 128:(cb + 1) * 128]) for cb in range(8)]) for lo in range(2)])
    w1 = np.asarray(inp["w_ffn_in"], f)
    w_f1 = np.stack([np.stack([chunk_rows(np.concatenate([w1[l][:, fb * 128:(fb + 1) * 128],
                                                          w1[l][:, DFF + fb * 128:DFF + (fb + 1) * 128]], 1))
                               for fb in range(NFB)]) for l in range(DEPTH)])
    w2 = np.asarray(inp["w_ffn_out"], f)
    w_f2 = np.stack([np.stack([chunk_rows(w2[l][:, cb * 128:(cb + 1) * 128]) for cb in range(8)]) for l in range(DEPTH)])
    return dict(prm=prm, w_sc=w_sc, w_at=w_at, w_fg=w_fg, w_o=w_o, w_p1=w_p1, w_p2=w_p2, w_f1=w_f1, w_f2=w_f2)


_NC_CACHE = {}


def kernel(**inputs):
    x = np.asarray(inputs["x"], np.float32)
    shared = prep_shared(inputs)
    key = "full"
    if key not in _NC_CACHE:
        _NC_CACHE[key] = build_program()
    nc = _NC_CACHE[key]
    in_maps = []
    for c in range(N_CORES):
        xs = x[c * SEQ_PER_CORE:(c + 1) * SEQ_PER_CORE]
        m = dict(shared)
        m["xT"] = np.ascontiguousarray(xs.transpose(0, 2, 1))
        in_maps.append(m)
    res = run_bass_kernel_spmd(nc, in_maps, core_ids=list(range(N_CORES)))
    out = np.empty_like(x)
    for c in range(N_CORES):
        out[c * SEQ_PER_CORE:(c + 1) * SEQ_PER_CORE] = res.results[c]["outT"].transpose(0, 2, 1)
    return out
```
